# Optimizing a Trainium2 kernel written in Bass

```python
import jax, jax.numpy as jnp
from jax import lax
import numpy as np

D_MODEL = 4096
BATCH = 2
SEQ = 4096
DEPTH = 1

D_RWKV = D_MODEL // 2
RWKV_HEAD_DIM = 64
RWKV_HEADS = D_RWKV // RWKV_HEAD_DIM
DECAY_LORA = max(32, int(round(1.8 * D_RWKV ** 0.5 / 32)) * 32)
ICLR_LORA = max(32, int(round(1.8 * D_RWKV ** 0.5 / 32)) * 32)
GATE_LORA = max(32, int(round(0.6 * D_RWKV ** 0.8 / 32)) * 32)
D_CONV = D_MODEL // 2
CONV_WIDTH = 3
D_FF = 4 * D_MODEL
RMS_EPS = 1e-5
LNX_EPS = 64e-5
L2_EPS = 1e-12

N_RWKV_COLS = 3 * D_RWKV + DECAY_LORA + ICLR_LORA + GATE_LORA
N_CONV_COLS = 3 * D_CONV
N_GATE_COLS = 2 * D_MODEL
N_IN_COLS = N_RWKV_COLS + N_CONV_COLS + N_GATE_COLS

kernel_name = "rwkv7_shortconv_gated_hybrid"


def rms_norm(x, g):
    xf = x.astype(jnp.float32)
    y = xf * lax.rsqrt(jnp.mean(xf * xf, axis=-1, keepdims=True) + RMS_EPS)
    return (y * g.astype(jnp.float32)).astype(x.dtype)


def token_shift(p):
    return jnp.pad(p[:, :-1], ((0, 0), (1, 0), (0, 0)))


def rwkv7_recurrence(r, decay, k, v, a, b):
    bsz, _, h, n = r.shape
    xs = tuple(jnp.swapaxes(t, 0, 1) for t in (r, decay, k, v, a, b))

    def step(S, inp):
        r_t, w_t, k_t, v_t, a_t, b_t = inp
        sa = jnp.einsum('bhij,bhj->bhi', S, a_t)
        S = S * w_t[:, :, None, :] + sa[..., None] * b_t[:, :, None, :] + v_t[..., None] * k_t[:, :, None, :]
        y_t = jnp.einsum('bhij,bhj->bhi', S, r_t)
        return S, y_t

    S0 = jnp.zeros((bsz, h, n, n), jnp.float32)
    _, ys = lax.scan(step, S0, xs)
    return jnp.swapaxes(ys, 0, 1)


def rwkv7_branch(p, shift_mu, w0, w_decay_up, a0, w_iclr_up, w_gate_up, k_k, k_a, r_k, lnx_w, lnx_b, w_out_a):
    bsz, t, _ = p.shape
    f32 = jnp.float32
    p_mix = p + (token_shift(p) - p) * shift_mu
    o1 = D_RWKV
    o2 = 2 * D_RWKV
    o3 = 3 * D_RWKV
    o4 = o3 + DECAY_LORA
    o5 = o4 + ICLR_LORA
    r, k, v, wd, ad, gd = jnp.split(p_mix, [o1, o2, o3, o4, o5], axis=-1)
    w_log = -jax.nn.softplus(-(w0 + jnp.tanh(wd) @ w_decay_up)) - 0.5
    decay = jnp.exp(-jnp.exp(w_log.astype(f32)))
    a = jax.nn.sigmoid(a0 + ad @ w_iclr_up)
    g = jax.nn.sigmoid(gd) @ w_gate_up
    hs = lambda z: z.astype(f32).reshape(bsz, t, RWKV_HEADS, RWKV_HEAD_DIM)
    kk = hs(k * k_k)
    kk = kk / jnp.maximum(jnp.sqrt(jnp.sum(kk * kk, axis=-1, keepdims=True)), L2_EPS)
    k = k * (1.0 + (a - 1.0) * k_a)
    r_h, k_h, v_h, a_h = hs(r), hs(k), hs(v), hs(a)
    decay_h = decay.reshape(bsz, t, RWKV_HEADS, RWKV_HEAD_DIM)
    y = rwkv7_recurrence(r_h, decay_h, k_h, v_h, -kk, kk * a_h)
    mu = jnp.mean(y, axis=-1, keepdims=True)
    var = jnp.mean(jnp.square(y - mu), axis=-1, keepdims=True)
    y = ((y - mu) * lax.rsqrt(var + LNX_EPS)).reshape(bsz, t, D_RWKV)
    y = y * lnx_w.astype(f32) + lnx_b.astype(f32)
    bonus = jnp.sum(r_h * k_h * r_k.astype(f32), axis=-1, keepdims=True) * v_h
    y = (y + bonus.reshape(bsz, t, D_RWKV)).astype(p.dtype)
    return (y * g) @ w_out_a


def short_conv_branch(p, conv_w, w_out_b):
    b_gate, c_gate, u = jnp.split(p, [D_CONV, 2 * D_CONV], axis=-1)
    z = c_gate * u
    z = lax.conv_general_dilated(
        z, conv_w[:, None, :], window_strides=(1,), padding=[(CONV_WIDTH - 1, 0)],
        dimension_numbers=('NWC', 'WIO', 'NWC'), feature_group_count=D_CONV)
    return (b_gate * z) @ w_out_b


def setup_inputs(seed: int = 0) -> dict:
    key = jax.random.key(seed)
    ks = jax.random.split(key, 24)
    f32 = jnp.float32
    nrm = lambda k, shape, scale: (jax.random.normal(k, shape, f32) * scale)
    L = DEPTH
    return {
        "x": nrm(ks[0], (BATCH, SEQ, D_MODEL), 1.0),
        "norm_mix_w": 1.0 + nrm(ks[1], (L, D_MODEL), 0.02),
        "w_in": nrm(ks[2], (L, D_MODEL, N_IN_COLS), D_MODEL ** -0.5),
        "gate_bias": nrm(ks[3], (L, N_GATE_COLS), 0.1),
        "shift_mu": jax.random.uniform(ks[4], (L, N_RWKV_COLS), f32),
        "w0": -2.0 + nrm(ks[5], (L, D_RWKV), 0.5),
        "w_decay_up": nrm(ks[6], (L, DECAY_LORA, D_RWKV), 0.1 * DECAY_LORA ** -0.5),
        "a0": nrm(ks[7], (L, D_RWKV), 0.5),
        "w_iclr_up": nrm(ks[8], (L, ICLR_LORA, D_RWKV), 0.1 * ICLR_LORA ** -0.5),
        "w_gate_up": nrm(ks[9], (L, GATE_LORA, D_RWKV), GATE_LORA ** -0.5),
        "k_k": 0.85 + nrm(ks[10], (L, D_RWKV), 0.05),
        "k_a": 1.0 + nrm(ks[11], (L, D_RWKV), 0.05),
        "r_k": 0.5 + nrm(ks[12], (L, RWKV_HEADS, RWKV_HEAD_DIM), 0.1),
        "lnx_w": 1.0 + nrm(ks[13], (L, D_RWKV), 0.02),
        "lnx_b": nrm(ks[14], (L, D_RWKV), 0.02),
        "w_out_a": nrm(ks[15], (L, D_RWKV, D_MODEL), D_RWKV ** -0.5),
        "conv_w": nrm(ks[16], (L, CONV_WIDTH, D_CONV), CONV_WIDTH ** -0.5),
        "w_out_b": nrm(ks[17], (L, D_CONV, D_MODEL), D_CONV ** -0.5),
        "w_out": nrm(ks[18], (L, D_MODEL, D_MODEL), D_MODEL ** -0.5),
        "norm_mlp_w": 1.0 + nrm(ks[19], (L, D_MODEL), 0.02),
        "w_mlp_up": nrm(ks[20], (L, D_MODEL, D_FF), D_MODEL ** -0.5),
        "w_mlp_down": nrm(ks[21], (L, D_FF, D_MODEL), D_FF ** -0.5),
        "norm_final_w": 1.0 + nrm(ks[22], (D_MODEL,), 0.02),
    }


def reference(x, norm_mix_w, w_in, gate_bias, shift_mu, w0, w_decay_up, a0, w_iclr_up, w_gate_up,
              k_k, k_a, r_k, lnx_w, lnx_b, w_out_a, conv_w, w_out_b, w_out,
              norm_mlp_w, w_mlp_up, w_mlp_down, norm_final_w):
    h = x
    for l in range(DEPTH):
        xn = rms_norm(h, norm_mix_w[l])
        p = xn @ w_in[l]
        p_rwkv, p_conv, p_gate = jnp.split(p, [N_RWKV_COLS, N_RWKV_COLS + N_CONV_COLS], axis=-1)
        y_a = rwkv7_branch(p_rwkv, shift_mu[l], w0[l], w_decay_up[l], a0[l], w_iclr_up[l], w_gate_up[l],
                           k_k[l], k_a[l], r_k[l], lnx_w[l], lnx_b[l], w_out_a[l])
        y_b = short_conv_branch(p_conv, conv_w[l], w_out_b[l])
        gates = jax.nn.sigmoid(p_gate + gate_bias[l])
        g_a, g_b = jnp.split(gates, 2, axis=-1)
        h = h + (g_a * y_a + g_b * y_b) @ w_out[l]
        hn = rms_norm(h, norm_mlp_w[l])
        h = h + jnp.square(jax.nn.relu(hn @ w_mlp_up[l])) @ w_mlp_down[l]
    return rms_norm(h, norm_final_w)
```

```python
import os
from contextlib import ExitStack
import numpy as np
import concourse.bass as bass
import concourse.mybir as mybir
from concourse.bass_utils import run_bass_kernel_spmd

F32 = mybir.dt.float32
BF16 = mybir.dt.bfloat16
ALU = mybir.AluOpType
AF = mybir.ActivationFunctionType

SAME_ENGINE_SYNC = True
EPOCH = 12000
D = 4096
T = 4096
NT = 1024
TB = 256
C = 64
NCH = TB // C
RMS_EPS = 1e-5
LNX_EPS = 64e-5
C0 = 0.6065306597126334


class Buf:
    __slots__ = ("w", "r")

    def __init__(self):
        self.w = None
        self.r = {}


class Eng:
    def __init__(self, fw, name, e, is_pe=False):
        self.fw, self.name, self.e, self.is_pe = fw, name, e, is_pe
        self.epoch = 0
        self.sem = fw.nc.alloc_semaphore(name=f"s_{name}_0")
        self.cnt = 0
        self.seen = {}
        self.pending = False

    def roll(self):
        if self.cnt >= EPOCH and not self.pending:
            self.epoch += 1
            self.sem = self.fw.nc.alloc_semaphore(name=f"s_{self.name}_{self.epoch}")
            self.cnt = 0


class FW:
    def __init__(self, nc, n_dma_sems=24):
        self.nc = nc
        self.E = {"pe": Eng(self, "pe", nc.tensor, True), "dve": Eng(self, "dve", nc.vector),
                  "act": Eng(self, "act", nc.scalar), "pool": Eng(self, "pool", nc.gpsimd),
                  "sp": Eng(self, "sp", nc.sync)}
        self.dsem = [nc.alloc_semaphore(name=f"s_dma_{i}") for i in range(n_dma_sems)]
        self.dcnt = [0] * n_dma_sems
        self.dgen = [0] * n_dma_sems
        self.dpool = {"sp": list(range(0, n_dma_sems // 2)), "pool": list(range(n_dma_sems // 2, n_dma_sems))}
        self.dnext = {"sp": 0, "pool": 0}
        self.nops = 0
        self.old = []

    def _wait(self, eng, tok):
        sem, val, key = tok
        if eng.seen.get(key, 0) >= val:
            return
        eng.e.wait_ge(sem, val)
        eng.seen[key] = val

    def _deps(self, eng, reads, writes):
        deps = []
        for b in reads:
            if b.w is not None:
                deps.append(b.w)
        for b in writes:
            if b.w is not None:
                deps.append(b.w)
            deps.extend(b.r.values())
        for tok in deps:
            key = tok[2]
            if key[0] == eng.name and (eng.is_pe or not SAME_ENGINE_SYNC):
                continue
            self._wait(eng, tok)

    def op(self, en, fn, reads=(), writes=(), inc=True):
        eng = self.E[en]
        self._deps(eng, reads, writes)
        ins = fn(eng.e)
        self.nops += 1
        if inc:
            eng.cnt += 1
            ins.then_inc(eng.sem, 1)
            eng.pending = False
            tok = (eng.sem, eng.cnt, (eng.name, eng.epoch))
        else:
            eng.pending = True
            tok = (eng.sem, eng.cnt + 1, (eng.name, eng.epoch))
        for b in writes:
            b.w = tok
            b.r = {}
        for b in reads:
            if b not in writes:
                b.r[tok[2]] = tok
        if inc:
            eng.roll()
        return tok

    def dma(self, en, out, in_, reads=(), writes=(), **kw):
        eng = self.E[en]
        self._deps(eng, reads, writes)
        pool_ = self.dpool[en]
        i = pool_[self.dnext[en]]
        self.dnext[en] = (self.dnext[en] + 1) % len(pool_)
        if self.dcnt[i] > 0:
            self._wait(eng, (self.dsem[i], self.dcnt[i], ("dma", i, self.dgen[i])))
        if self.dcnt[i] >= 16 * 2000:
            self.old.append((self.dsem[i], self.dcnt[i], ("dma", i, self.dgen[i])))
            self.dsem[i] = self.nc.alloc_semaphore(name=f"s_dma_{i}_{self.nops}")
            self.dcnt[i] = 0
            self.dgen[i] += 1
        ins = eng.e.dma_start(out=out, in_=in_, **kw)
        self.nops += 1
        self.dcnt[i] += 16
        ins.then_inc(self.dsem[i], 16)
        tok = (self.dsem[i], self.dcnt[i], ("dma", i, self.dgen[i]))
        for b in writes:
            b.w = tok
            b.r = {}
        for b in reads:
            b.r[tok[2]] = tok
        return tok

    def barrier(self):
        toks = []
        for o in self.E.values():
            assert not o.pending
            if o.cnt > 0:
                toks.append((o.sem, o.cnt, (o.name, o.epoch)))
        for i in range(len(self.dsem)):
            if self.dcnt[i] > 0:
                toks.append((self.dsem[i], self.dcnt[i], ("dma", i, self.dgen[i])))
        for e in self.E.values():
            for t in toks:
                if t[2][0] == e.name:
                    continue
                self._wait(e, t)


def build_program(dbg=False):
    skip = bool(os.environ.get('KSKIP'))
    SK = (lambda n: 0) if skip else (lambda n: n)
    nc = bass.Bass("TRN2", target_bir_lowering=False)
    fw = FW(nc)

    def din(name, shape, dt=F32):
        return nc.dram_tensor(name, list(shape), dt, kind="ExternalInput").ap()

    xseq = din("xseq", [T, D])
    xown = din("xown", [NT, D])
    xprev = din("xprev", [128, D])
    nmix = din("nmix", [1, D])
    nmlp = din("nmlp", [1, D])
    nfin = din("nfin", [1, D])
    w_rwkv = din("w_rwkv", [D, 1984])
    mu_hj = din("mu_hj", [64, 24])
    mu_wd = din("mu_wd", [96, 1])
    mu_ad = din("mu_ad", [96, 1])
    mu_gd = din("mu_gd", [128, 2])
    p_hj = din("p_hj", [64, 8 * 8])
    wdu = din("wdu", [96, 512])
    wau = din("wau", [96, 512])
    wgu = din("wgu", [256, 512])
    w_conv = din("w_conv", [D, 6144])
    w_gate = din("w_gate", [D, 8192])
    gbias = din("gbias", [128, 64])
    convw = din("convw", [128, 48])
    w_oa = din("w_oa", [2048, D])
    w_ob = din("w_ob", [2048, D])
    w_o = din("w_o", [D, D])
    w_up = din("w_up", [D, 4 * D])
    w_dn = din("w_dn", [4 * D, D])
    sel = din("sel", [128, 8])
    out = nc.dram_tensor("out", [NT, D], F32, kind="ExternalOutput").ap()
    ysend = nc.dram_tensor("ysend", [512, T], BF16)
    ygath = nc.dram_tensor("ygath", [4096, T], BF16)
    mscr = nc.dram_tensor("mscr", [D, NT], BF16)
    wa_s = nc.dram_tensor("wa_s", [16, 128, 32, 128], BF16)
    wo_s = nc.dram_tensor("wo_s", [8, 128, 32, 512], BF16)
    wup_s = nc.dram_tensor("wup_s", [64, 128, 32, 256], BF16)
    wdn_s = nc.dram_tensor("wdn_s", [64, 128, 2, D], BF16)
    Bwa_s = [Buf() for _ in range(16)]; Bwo_s = [Buf() for _ in range(8)]; Bwup_s = [Buf() for _ in range(64)]; Bwdn_s = [Buf() for _ in range(64)]
    BYS, BYG, BMS = Buf(), Buf(), Buf()
    if dbg in (1, 2):
        dbg_y = nc.dram_tensor("dbg_y", [512, T], BF16, kind="ExternalOutput").ap()
        dbg_m = nc.dram_tensor("dbg_m", [D, NT], BF16, kind="ExternalOutput").ap()

    def op(en, fn, r=(), w=(), inc=True):
        return fw.op(en, fn, reads=r, writes=w, inc=inc)

    PROF = bool(os.environ.get("KPROF"))
    cur_scope = [None]

    def scope(name):
        if not PROF:
            return
        if cur_scope[0] is not None:
            nc.leave_named_scope(cur_scope[0][0], cur_scope[0][1], False)
        cur_scope[0] = None
        if name is not None:
            sid, _ = nc.enter_named_scope(name, False)
            cur_scope[0] = (name, sid)

    ev = [0]

    def evac(out_ap, in_ap, r, w):
        ev[0] += 1
        if ev[0] % 2:
            return op("dve", lambda e: e.tensor_copy(out_ap, in_ap), r, w)
        return op("act", lambda e: e.copy(out_ap, in_ap), r, w)

    with ExitStack() as g_es:
        sbt = lambda es, name, shape, dt=F32: es.enter_context(nc.sbuf_tensor(name, list(shape), dt))
        pst = lambda es, name, shape, dt=F32: es.enter_context(nc.psum_tensor(name, list(shape), dt))
        identf = sbt(g_es, "identf", [128, 128])
        ident = sbt(g_es, "ident", [128, 128], BF16)
        gb = sbt(g_es, "gb", [128, D])
        small = sbt(g_es, "small", [128, 16])
        Bident, Bgb, Bsmall = Buf(), Buf(), Buf()
        op("pool", lambda e: e.memset(identf[:], 0.0), w=[Bident])
        op("pool", lambda e: e.affine_select(out=identf[:], in_=identf[:], pattern=[[-1, 128]], compare_op=ALU.not_equal,
                                             fill=1.0, base=0, channel_multiplier=1), r=[Bident], w=[Bident])
        op("dve", lambda e: e.tensor_copy(ident[:], identf[:]), r=[Bident], w=[Bident])

        def load_gb(src):
            fw.dma("sp", gb[:], src.to_broadcast([128, D]), writes=[Bgb])

        def norm_T(es_bufs, src_rows_ap, dst_fn, ncols_keep=128, col0=0):
            xt, Bxt, xs, Bxs, pT, BpT, st, Bst = es_bufs
            fw.dma("sp", xt[:], src_rows_ap, writes=[Bxt])
            norm_T_sb(es_bufs, dst_fn, ncols_keep, col0)

        def norm_T_sb(es_bufs, dst_fn, ncols_keep=128, col0=0, eps=RMS_EPS):
            xt, Bxt, xs, Bxs, pT, BpT, st, Bst = es_bufs
            Bxt = list(Bxt) if isinstance(Bxt, list) else [Bxt]
            xt = xt[:]
            op("act", lambda e: e.activation(xs[:], xt, AF.Square, accum_out=st[:, 0:1]), r=Bxt, w=[Bxs, Bst])
            op("dve", lambda e: e.tensor_scalar(st[:, 1:2], st[:, 0:1], 1.0 / D, eps, ALU.mult, ALU.add), r=[Bst], w=[Bst])
            op("act", lambda e: e.activation(st[:, 2:3], st[:, 1:2], AF.Sqrt), r=[Bst], w=[Bst])
            op("dve", lambda e: e.reciprocal(st[:, 3:4], st[:, 2:3]), r=[Bst], w=[Bst])
            op("dve", lambda e: e.scalar_tensor_tensor(xs[:], xt, st[:, 3:4], gb[:], ALU.mult, ALU.mult), r=Bxt + [Bst, Bgb, Bxs], w=[Bxs])
            for k0 in range(0, 32, 4):
                pt, Bpt = pT[(k0 // 4) % 2], BpT[(k0 // 4) % 2]
                for kk in range(4):
                    k = k0 + kk
                    op("pe", lambda e: e.matmul(pt[:, kk * 128:(kk + 1) * 128], lhsT=xs[:, k * 128:(k + 1) * 128], rhs=ident[:],
                                                start=True, stop=True), r=[Bxs, Bident], w=[Bpt], inc=(kk == 3))
                dst, Bdst = dst_fn(k0, 4)
                src = pt[:].rearrange("p (k t) -> p k t", t=128)[:, :, col0:col0 + ncols_keep]
                evac(dst, src, [Bpt], [Bdst])

        scope("A")
        with ExitStack() as es:
            load_gb(nmix)
            xt = sbt(es, "a_xt", [128, D]); xs = sbt(es, "a_xs", [128, D], BF16)
            st = sbt(es, "a_st", [128, 8])
            pT = [pst(es, f"a_pT{i}", [128, 512]) for i in range(2)]
            nbufs = (xt, Buf(), xs, Buf(), pT, [Buf(), Buf()], st, Buf())
            xnT = sbt(es, "a_xnT", [128, 32, TB], BF16); BxnT = Buf()
            wst = [sbt(es, f"a_wst{i}", [128, 32, 128], BF16) for i in range(2)]; Bwst = [Buf(), Buf()]
            pin = [pst(es, f"a_pin{i}", [128, 512]) for i in range(2)]; Bpin = [Buf(), Buf()]
            pmisc = [pst(es, f"a_pm{i}", [128, 512]) for i in range(4)]; Bpm = [Buf() for _ in range(4)]
            praw = sbt(es, "a_praw", [64, 24, TB + 1]); Bpraw = Buf()
            lraw = sbt(es, "a_lraw", [128, 4, TB + 1]); Blraw = Buf()
            carry = sbt(es, "a_carry", [128, 28]); Bcarry = Buf()
            S = [sbt(es, f"a_S{i}", [64, 8, TB]) for i in range(6)]; BS = [Buf() for _ in range(6)]
            AR = sbt(es, "a_AR", [64, 8, NCH, 2 * C], BF16); BAR = Buf()
            Bt = sbt(es, "a_Bt", [64, 8, TB], BF16); BBt = Buf()
            Kt = sbt(es, "a_Kt", [64, 8, TB], BF16); BKt = Buf()
            lact = sbt(es, "a_lact", [128, 4, TB], BF16); Blact = Buf()
            prm = sbt(es, "a_prm", [64, 64]); Bprm = Buf()
            muhj = sbt(es, "a_muhj", [64, 24]); mul_ = sbt(es, "a_mul", [128, 4]); Bmu = Buf()
            lw = sbt(es, "a_lw", [128, 2, 512], BF16); lwg = sbt(es, "a_lwg", [128, 2, 512], BF16); Blw = Buf()
            ones = sbt(es, "a_ones", [64, 64]); Bones = Buf()
            mG = sbt(es, "a_mG", [64, 2 * C]); mL = sbt(es, "a_mL", [64, C]); rmask = sbt(es, "a_rmask", [64, TB]); Bmask = Buf()
            H = sbt(es, "a_H", [64, 8, C]); Hb = sbt(es, "a_Hb", [64, 8, C], BF16); BH = Buf(); BHb = Buf()
            names = ["LabT", "MrbT", "LakT", "MrkT", "L1", "N1", "L2", "N2", "P", "P2", "Btok", "Ktok", "Vtok", "X0", "U"]
            RB = {n: (sbt(es, "a_" + n, [64, 8, C], BF16), Buf()) for n in names}
            yT = S[5]; ByT = BS[5]
            ysb = sbt(es, "a_ysb", [64, 8, TB], BF16); Bysb = Buf()
            ltmp = sbt(es, "a_ltmp", [128, 4, TB]); Bltmp = Buf()

            fw.dma("sp", prm[:], p_hj, writes=[Bprm])
            fw.dma("sp", muhj[:], mu_hj, writes=[Bmu])
            op("dve", lambda e: e.tensor_scalar(prm[:, 32:40], prm[:, 24:32], -1.0, 1.0, ALU.mult, ALU.add), r=[Bprm], w=[Bprm])
            op("pool", lambda e: e.memset(mul_[:], 0.0), w=[Bmu])
            fw.dma("sp", mul_[0:96, 0:1], mu_wd, writes=[Bmu])
            fw.dma("sp", mul_[0:96, 1:2], mu_ad, writes=[Bmu])
            fw.dma("sp", mul_[:, 2:4], mu_gd, writes=[Bmu])
            fw.dma("pool", lw[0:96, 0, :], wdu, writes=[Blw])
            fw.dma("pool", lw[0:96, 1, :], wau, writes=[Blw])
            fw.dma("pool", lwg[:], wgu.rearrange("(k p) n -> p k n", p=128), writes=[Blw])
            op("dve", lambda e: e.memset(ones[:], 1.0), w=[Bones])
            for i_ in range(2):
                op("pool", lambda e: e.memset(wst[i_][:], 0.0), w=[Bwst[i_]])
            op("pool", lambda e: e.memset(mG[:], 1.0), w=[Bmask])
            op("pool", lambda e: e.affine_select(out=mG[:, 0:C], in_=mG[:, 0:C], pattern=[[1, C]], compare_op=ALU.is_gt, fill=0.0,
                                                 base=0, channel_multiplier=-1), r=[Bmask], w=[Bmask])
            op("pool", lambda e: e.affine_select(out=mG[:, C:2 * C], in_=mG[:, C:2 * C], pattern=[[1, C]], compare_op=ALU.is_ge, fill=0.0,
                                                 base=0, channel_multiplier=-1), r=[Bmask], w=[Bmask])
            op("pool", lambda e: e.memset(mL[:], 1.0), r=[Bmask], w=[Bmask])
            op("pool", lambda e: e.affine_select(out=mL[:], in_=mL[:], pattern=[[-1, C]], compare_op=ALU.is_gt, fill=0.0,
                                                 base=0, channel_multiplier=1), r=[Bmask], w=[Bmask])
            op("pool", lambda e: e.memset(rmask[:], 1.0), r=[Bmask], w=[Bmask])
            op("pool", lambda e: e.memset(rmask[:].rearrange("p (c t) -> p c t", t=C)[:, :, 0:1], 0.0), r=[Bmask], w=[Bmask])
            op("dve", lambda e: e.memset(H[:], 0.0), w=[BH])
            op("dve", lambda e: e.memset(Hb[:], 0.0), w=[BHb])
            op("dve", lambda e: e.memset(carry[:], 0.0), w=[Bcarry])
            op("dve", lambda e: e.memset(lraw[:], 0.0), w=[Blraw])
            op("dve", lambda e: e.memset(lact[:], 0.0), w=[Blact])

            def prm_b(i):
                return prm[:, i * 8:(i + 1) * 8].unsqueeze(2).to_broadcast([64, 8, TB])

            for tb in range(min(int(os.environ.get('KNTB', '999')), SK(T // TB))):
                t0 = tb * TB
                for tt in range(TB // 128):
                    norm_T(nbufs, xseq[t0 + tt * 128:t0 + (tt + 1) * 128, :],
                           lambda k0, nk, tt=tt: (xnT[:, k0:k0 + nk, tt * 128:(tt + 1) * 128], BxnT))
                for gi in range(16):
                    wb, Bwb = wst[gi % 2], Bwst[gi % 2]
                    if gi < 12:
                        c0, ncol = gi * 128, 128
                    elif gi == 12:
                        c0, ncol = 1536, 96
                    elif gi == 13:
                        c0, ncol = 1632, 96
                    else:
                        c0, ncol = 1728 + (gi - 14) * 128, 128
                    if tb == 0:
                        fw.dma("pool", wb[:, :, 0:ncol], w_rwkv[:, c0:c0 + ncol].rearrange("(k p) n -> p k n", p=128), writes=[Bwb])
                        fw.dma("sp", wa_s.ap()[gi], wb[:], reads=[Bwb], writes=[Bwa_s[gi]])
                    else:
                        fw.dma("sp", wb[:], wa_s.ap()[gi], reads=[Bwa_s[gi]], writes=[Bwb])
                    if gi < 12:
                        for hh in range(2):
                            pp, Bpp = pin[hh], Bpin[hh]
                            for k in range(32):
                                op("pe", lambda e: e.matmul(pp[0:64, 0:TB], lhsT=wb[:, k, hh * 64:(hh + 1) * 64], rhs=xnT[:, k, :],
                                                            start=(k == 0), stop=(k == 31)), r=[Bwb, BxnT], w=[Bpp], inc=(k == 31))
                            grp = gi * 2 + hh
                            evac(praw[:, grp, 1:TB + 1], pp[0:64, 0:TB], [Bpp], [Bpraw])
                    else:
                        pp, Bpp = pin[gi % 2], Bpin[gi % 2]
                        for k in range(32):
                            op("pe", lambda e: e.matmul(pp[0:ncol, 0:TB], lhsT=wb[:, k, 0:ncol], rhs=xnT[:, k, :],
                                                        start=(k == 0), stop=(k == 31)), r=[Bwb, BxnT], w=[Bpp], inc=(k == 31))
                        evac(lraw[0:ncol, gi - 12, 1:TB + 1], pp[0:ncol, 0:TB], [Bpp], [Blraw])
                op("dve", lambda e: e.tensor_copy(praw[:, :, 0:1], carry[0:64, 0:24].unsqueeze(2)), r=[Bcarry, Bpraw], w=[Bpraw])
                op("dve", lambda e: e.tensor_copy(lraw[:, :, 0:1], carry[:, 24:28].unsqueeze(2)), r=[Bcarry, Blraw], w=[Blraw])
                op("dve", lambda e: e.tensor_copy(carry[0:64, 0:24].unsqueeze(2), praw[:, :, TB:TB + 1]), r=[Bpraw, Bcarry], w=[Bcarry])
                op("dve", lambda e: e.tensor_copy(carry[:, 24:28].unsqueeze(2), lraw[:, :, TB:TB + 1]), r=[Blraw, Bcarry], w=[Bcarry])
                for fam in range(3):
                    cur = praw[:, fam * 8:(fam + 1) * 8, 1:TB + 1]
                    prv = praw[:, fam * 8:(fam + 1) * 8, 0:TB]
                    mub = muhj[:, fam * 8:(fam + 1) * 8].unsqueeze(2).to_broadcast([64, 8, TB])
                    op("dve", lambda e: e.tensor_tensor(S[0][:], prv, cur, ALU.subtract), r=[Bpraw], w=[BS[0]])
                    op("pool", lambda e: e.tensor_tensor(S[0][:], S[0][:], mub, ALU.mult), r=[BS[0], Bmu], w=[BS[0]])
                    op("dve", lambda e: e.tensor_tensor(cur, cur, S[0][:], ALU.add), r=[BS[0], Bpraw], w=[Bpraw])
                lcur = lraw[:, :, 1:TB + 1]; lprv = lraw[:, :, 0:TB]
                op("dve", lambda e: e.tensor_tensor(ltmp[:], lprv, lcur, ALU.subtract), r=[Blraw], w=[Bltmp])
                op("pool", lambda e: e.tensor_tensor(ltmp[:], ltmp[:], mul_[:].unsqueeze(2).to_broadcast([128, 4, TB]), ALU.mult), r=[Bltmp, Bmu], w=[Bltmp])
                op("dve", lambda e: e.tensor_tensor(lcur, lcur, ltmp[:], ALU.add), r=[Bltmp, Blraw], w=[Blraw])
                rr = praw[:, 0:8, 1:TB + 1]; kr = praw[:, 8:16, 1:TB + 1]; vr = praw[:, 16:24, 1:TB + 1]
                op("act", lambda e: e.activation(lact[0:96, 0, :], lraw[0:96, 0, 1:TB + 1], AF.Tanh), r=[Blraw], w=[Blact])
                op("act", lambda e: e.copy(lact[0:96, 1, :], lraw[0:96, 1, 1:TB + 1]), r=[Blraw], w=[Blact])
                op("act", lambda e: e.activation(lact[:, 2:4, :], lraw[:, 2:4, 1:TB + 1], AF.Sigmoid), r=[Blraw], w=[Blact])
                for (li, pidx, dst, Bd) in ((0, 0, S[1], BS[1]), (1, 1, S[3], BS[3])):
                    for h in range(8):
                        pm_, Bp_ = pmisc[h % 4], Bpm[h % 4]
                        op("pe", lambda e: e.matmul(pm_[0:64, 0:TB], lhsT=lw[0:96, li, h * 64:(h + 1) * 64], rhs=lact[0:96, li, :],
                                                    start=True, stop=True), r=[Blw, Blact], w=[Bp_])
                        op("act", lambda e: e.activation(dst[:, h, :], pm_[0:64, 0:TB], AF.Sigmoid, bias=prm[:, pidx * 8 + h:pidx * 8 + h + 1], scale=1.0),
                           r=[Bp_, Bprm], w=[Bd])
                for h in range(8):
                    op("dve", lambda e: e.tensor_tensor_scan(S[2][:, h, :], rmask[:], S[1][:, h, :], 0.0, ALU.mult, ALU.add), r=[Bmask, BS[1]], w=[BS[2]])
                op("dve", lambda e: e.tensor_tensor(S[1][:], S[2][:], S[1][:], ALU.subtract), r=[BS[1], BS[2]], w=[BS[1]])
                op("act", lambda e: e.activation(S[1][:], S[1][:], AF.Exp, scale=-C0), r=[BS[1]], w=[BS[1]])
                op("act", lambda e: e.activation(S[4][:], S[2][:], AF.Exp, scale=-C0), r=[BS[2]], w=[BS[4]])
                op("act", lambda e: e.activation(S[2][:], S[2][:], AF.Exp, scale=C0), r=[BS[2]], w=[BS[2]])
                op("pool", lambda e: e.tensor_tensor(S[0][:], kr, prm_b(2), ALU.mult), r=[Bpraw, Bprm], w=[BS[0]])
                op("dve", lambda e: e.tensor_tensor(S[5][:], S[0][:], S[0][:], ALU.mult), r=[BS[0]], w=[BS[5]])
                for h in range(8):
                    pm_, Bp_ = pmisc[h % 4], Bpm[h % 4]
                    op("pe", lambda e: e.matmul(pm_[0:64, 0:TB], lhsT=ones[:], rhs=S[5][:, h, :], start=True, stop=True), r=[Bones, BS[5]], w=[Bp_])
                    op("act", lambda e: e.activation(S[5][:, h, :], pm_[0:64, 0:TB], AF.Sqrt), r=[Bp_, BS[5]], w=[BS[5]])
                op("dve", lambda e: e.tensor_scalar_max(S[5][:], S[5][:], 1e-12), r=[BS[5]], w=[BS[5]])
                op("dve", lambda e: e.reciprocal(S[5][:], S[5][:]), r=[BS[5]], w=[BS[5]])
                op("dve", lambda e: e.tensor_tensor(S[0][:], S[0][:], S[5][:], ALU.mult), r=[BS[0], BS[5]], w=[BS[0]])
                op("dve", lambda e: e.scalar_tensor_tensor(S[5][:], S[0][:], -1.0, S[1][:], ALU.mult, ALU.mult), r=[BS[0], BS[1], BS[5]], w=[BS[5]])
                op("dve", lambda e: e.tensor_copy(AR[:, :, :, 0:C], S[5][:].rearrange("p h (c t) -> p h c t", t=C)), r=[BS[5], BAR], w=[BAR])
                op("dve", lambda e: e.tensor_tensor(S[5][:], S[0][:], S[3][:], ALU.mult), r=[BS[0], BS[3], BS[5]], w=[BS[5]])
                op("dve", lambda e: e.tensor_tensor(Bt[:], S[5][:], S[2][:], ALU.mult), r=[BS[5], BS[2], BBt], w=[BBt])
                op("pool", lambda e: e.tensor_tensor(S[5][:], S[3][:], prm_b(3), ALU.mult), r=[BS[3], Bprm, BS[5]], w=[BS[5]])
                op("pool", lambda e: e.tensor_tensor(S[5][:], S[5][:], prm_b(4), ALU.add), r=[BS[5], Bprm], w=[BS[5]])
                op("dve", lambda e: e.tensor_tensor(S[0][:], kr, S[5][:], ALU.mult), r=[Bpraw, BS[5], BS[0]], w=[BS[0]])
                op("dve", lambda e: e.tensor_tensor(Kt[:], S[0][:], S[2][:], ALU.mult), r=[BS[0], BS[2], BKt], w=[BKt])
                op("dve", lambda e: e.tensor_tensor(AR[:, :, :, C:2 * C], rr.rearrange("p h (c t) -> p h c t", t=C),
                                                    S[4][:].rearrange("p h (c t) -> p h c t", t=C), ALU.mult), r=[Bpraw, BS[4], BAR], w=[BAR])
                op("pool", lambda e: e.tensor_tensor(S[5][:], rr, prm_b(5), ALU.mult), r=[Bpraw, Bprm, BS[5]], w=[BS[5]])
                op("dve", lambda e: e.tensor_tensor(S[5][:], S[5][:], S[0][:], ALU.mult), r=[BS[5], BS[0]], w=[BS[5]])
                for h in range(8):
                    pm_, Bp_ = pmisc[h % 4], Bpm[h % 4]
                    op("pe", lambda e: e.matmul(pm_[0:64, 0:TB], lhsT=ones[:], rhs=S[5][:, h, :], start=True, stop=True), r=[Bones, BS[5]], w=[Bp_])
                    op("dve", lambda e: e.tensor_tensor(S[3][:, h, :], pm_[0:64, 0:TB], vr[:, h, :], ALU.mult), r=[Bp_, Bpraw, BS[3]], w=[BS[3]])
                for c in range(NCH):
                    tsl = slice(c * C, (c + 1) * C)
                    gA = pmisc[0]; gB = pmisc[1]; gC = pmisc[2]; gD = pmisc[3]
                    for (lh, pa, pb, Bpa, Bpb, Blh) in ((Bt, gA, gB, Bpm[0], Bpm[1], BBt), (Kt, gC, gD, Bpm[2], Bpm[3], BKt)):
                        for h in range(8):
                            pdst, Bpd = (pa, Bpa) if h < 4 else (pb, Bpb)
                            op("pe", lambda e: e.matmul(pdst[0:64, (h % 4) * 128:(h % 4 + 1) * 128], lhsT=lh[:, h, tsl], rhs=AR[:, h, c, :],
                                                        start=True, stop=True), r=[Blh, BAR], w=[Bpd], inc=(h % 4 == 3))
                    mGb = mG[:].unsqueeze(1).to_broadcast([64, 4, 2 * C])
                    for (pa, Bpa, hs, nL, nM) in ((gA, Bpm[0], 0, "LabT", "MrbT"), (gB, Bpm[1], 4, "LabT", "MrbT"),
                                                  (gC, Bpm[2], 0, "LakT", "MrkT"), (gD, Bpm[3], 4, "LakT", "MrkT")):
                        pv = pa[0:64, :].rearrange("p (h x) -> p h x", x=2 * C)
                        op("dve", lambda e: e.tensor_tensor(RB[nL][0][:, hs:hs + 4, :], pv[:, :, 0:C], mGb[:, :, 0:C], ALU.mult),
                           r=[Bpa, Bmask, RB[nL][1]], w=[RB[nL][1]])
                        op("dve", lambda e: e.tensor_tensor(RB[nM][0][:, hs:hs + 4, :], pv[:, :, C:2 * C], mGb[:, :, C:2 * C], ALU.mult),
                           r=[Bpa, Bmask, RB[nM][1]], w=[RB[nM][1]])
                    pL, BpL = pin[0], Bpin[0]
                    for h in range(8):
                        op("pe", lambda e: e.matmul(pL[0:64, h * C:(h + 1) * C], lhsT=AR[:, h, c, 0:C], rhs=Bt[:, h, tsl], start=True, stop=True),
                           r=[BAR, BBt], w=[BpL], inc=(h == 7))
                    op("dve", lambda e: e.tensor_tensor(RB["L1"][0][:], pL[0:64, :].rearrange("p (h s) -> p h s", s=C),
                                                        mL[:].unsqueeze(1).to_broadcast([64, 8, C]), ALU.mult), r=[BpL, Bmask, RB["L1"][1]], w=[RB["L1"][1]])
                    for (src_fn, Bsrc, nm, use_f32) in ((lambda h: Bt[:, h, tsl], BBt, "Btok", False), (lambda h: Kt[:, h, tsl], BKt, "Ktok", False)):
                        pt_, Bpt_ = pin[1], Bpin[1]
                        for h in range(8):
                            op("pe", lambda e: e.matmul(pt_[0:64, h * C:(h + 1) * C], lhsT=src_fn(h), rhs=ident[0:64, 0:64], start=True, stop=True),
                               r=[Bsrc, Bident], w=[Bpt_], inc=(h == 7))
                        evac(RB[nm][0][:], pt_[0:64, :].rearrange("p (h s) -> p h s", s=C), [Bpt_, RB[nm][1]], [RB[nm][1]])
                    pt_, Bpt_ = pin[1], Bpin[1]
                    for h in range(8):
                        op("pe", lambda e: e.matmul(pt_[0:64, h * C:(h + 1) * C], lhsT=vr[:, h, tsl], rhs=identf[0:64, 0:64], start=True, stop=True),
                           r=[Bpraw, Bident], w=[Bpt_], inc=(h == 7))
                    evac(RB["Vtok"][0][:], pt_[0:64, :].rearrange("p (h s) -> p h s", s=C), [Bpt_, RB["Vtok"][1]], [RB["Vtok"][1]])
                    Ncur, Lcur = "LabT", "L1"
                    Pc, Pn = "P", "P2"
                    op("dve", lambda e: e.tensor_tensor(RB[Pc][0][:], RB["LabT"][0][:], ident[0:64, 0:64].unsqueeze(1).to_broadcast([64, 8, C]), ALU.add),
                       r=[RB["LabT"][1], Bident, RB[Pc][1]], w=[RB[Pc][1]])
                    for lvl in range(1, 6):
                        Lnew = "L2" if Lcur == "L1" else "L1"
                        Nnew = "N2" if Ncur in ("LabT", "N1") else "N1"
                        pq, Bpq = pin[0], Bpin[0]
                        for h in range(8):
                            op("pe", lambda e: e.matmul(pq[0:64, h * C:(h + 1) * C], lhsT=RB[Ncur][0][:, h, :], rhs=RB[Lcur][0][:, h, :], start=True, stop=True),
                               r=[RB[Ncur][1], RB[Lcur][1]], w=[Bpq], inc=(h == 7))
                        if lvl < 5:
                            pr_, Bpr_ = pin[1], Bpin[1]
                            for h in range(8):
                                op("pe", lambda e: e.matmul(pr_[0:64, h * C:(h + 1) * C], lhsT=RB[Lcur][0][:, h, :], rhs=RB[Ncur][0][:, h, :], start=True, stop=True),
                                   r=[RB[Ncur][1], RB[Lcur][1]], w=[Bpr_], inc=(h == 7))
                        evac(RB[Lnew][0][:], pq[0:64, :].rearrange("p (h s) -> p h s", s=C), [Bpq, RB[Lnew][1]], [RB[Lnew][1]])
                        if lvl < 5:
                            evac(RB[Nnew][0][:], pr_[0:64, :].rearrange("p (h s) -> p h s", s=C), [Bpr_, RB[Nnew][1]], [RB[Nnew][1]])
                        pq2, Bpq2 = pmisc[lvl % 4], Bpm[lvl % 4]
                        for h in range(8):
                            op("pe", lambda e: e.matmul(pq2[0:64, h * C:(h + 1) * C], lhsT=RB[Lnew][0][:, h, :], rhs=RB[Pc][0][:, h, :], start=True, stop=True),
                               r=[RB[Lnew][1], RB[Pc][1]], w=[Bpq2], inc=(h == 7))
                        op("dve", lambda e: e.tensor_tensor(RB[Pn][0][:], pq2[0:64, :].rearrange("p (h s) -> p h s", s=C), RB[Pc][0][:], ALU.add),
                           r=[Bpq2, RB[Pc][1], RB[Pn][1]], w=[RB[Pn][1]])
                        Pc, Pn = Pn, Pc
                        Lcur = Lnew
                        if lvl < 5:
                            Ncur = Nnew
                    TT = Pc
                    px, Bpx = pmisc[0], Bpm[0]
                    for h in range(8):
                        op("pe", lambda e: e.matmul(px[0:64, h * C:(h + 1) * C], lhsT=AR[:, h, c, 0:C], rhs=Hb[:, h, :], start=True, stop=False),
                           r=[BAR, BHb], w=[Bpx], inc=False)
                        op("pe", lambda e: e.matmul(px[0:64, h * C:(h + 1) * C], lhsT=RB["LakT"][0][:, h, :], rhs=RB["Vtok"][0][:, h, :], start=False, stop=True),
                           r=[RB["LakT"][1], RB["Vtok"][1]], w=[Bpx], inc=(h == 7))
                    evac(RB["X0"][0][:], px[0:64, :].rearrange("p (h s) -> p h s", s=C), [Bpx, RB["X0"][1]], [RB["X0"][1]])
                    pu, Bpu = pmisc[1], Bpm[1]
                    for h in range(8):
                        op("pe", lambda e: e.matmul(pu[0:64, h * C:(h + 1) * C], lhsT=RB[TT][0][:, h, :], rhs=RB["X0"][0][:, h, :], start=True, stop=True),
                           r=[RB[TT][1], RB["X0"][1]], w=[Bpu], inc=(h == 7))
                    evac(RB["U"][0][:], pu[0:64, :].rearrange("p (h s) -> p h s", s=C), [Bpu, RB["U"][1]], [RB["U"][1]])
                    py, Bpy = pmisc[2], Bpm[2]
                    for h in range(8):
                        op("pe", lambda e: e.matmul(py[0:64, h * C:(h + 1) * C], lhsT=Hb[:, h, :], rhs=AR[:, h, c, C:2 * C], start=True, stop=False),
                           r=[BAR, BHb], w=[Bpy], inc=False)
                        op("pe", lambda e: e.matmul(py[0:64, h * C:(h + 1) * C], lhsT=RB["U"][0][:, h, :], rhs=RB["MrbT"][0][:, h, :], start=False, stop=False),
                           r=[RB["U"][1], RB["MrbT"][1]], w=[Bpy], inc=False)
                        op("pe", lambda e: e.matmul(py[0:64, h * C:(h + 1) * C], lhsT=RB["Vtok"][0][:, h, :], rhs=RB["MrkT"][0][:, h, :], start=False, stop=True),
                           r=[RB["Vtok"][1], RB["MrkT"][1]], w=[Bpy], inc=(h == 7))
                    evac(yT[:, :, tsl], py[0:64, :].rearrange("p (h s) -> p h s", s=C), [Bpy, ByT], [ByT])
                    ph_, Bph = pmisc[3], Bpm[3]
                    for h in range(8):
                        op("pe", lambda e: e.matmul(ph_[0:64, h * C:(h + 1) * C], lhsT=RB["Btok"][0][:, h, :], rhs=RB["U"][0][:, h, :], start=True, stop=False),
                           r=[RB["Btok"][1], RB["U"][1]], w=[Bph], inc=False)
                        op("pe", lambda e: e.matmul(ph_[0:64, h * C:(h + 1) * C], lhsT=RB["Ktok"][0][:, h, :], rhs=RB["Vtok"][0][:, h, :], start=False, stop=True),
                           r=[RB["Ktok"][1], RB["Vtok"][1]], w=[Bph], inc=(h == 7))
                    op("dve", lambda e: e.tensor_tensor(H[:], H[:], ph_[0:64, :].rearrange("p (h s) -> p h s", s=C), ALU.add), r=[Bph, BH], w=[BH])
                    pcb = S[4][:, :, c * C + C - 1:c * C + C].to_broadcast([64, 8, C])
                    op("dve", lambda e: e.tensor_tensor(H[:], H[:], pcb, ALU.mult), r=[BH, BS[4]], w=[BH])
                    op("act", lambda e: e.copy(Hb[:], H[:]), r=[BH, BHb], w=[BHb])
                for h in range(8):
                    pm_, Bp_ = pmisc[h % 4], Bpm[h % 4]
                    for kk2 in range(2):
                        op("pe", lambda e: e.matmul(pm_[0:64, 0:TB], lhsT=lwg[:, kk2, h * 64:(h + 1) * 64], rhs=lact[:, 2 + kk2, :],
                                                    start=(kk2 == 0), stop=(kk2 == 1)), r=[Blw, Blact], w=[Bp_], inc=(kk2 == 1))
                    evac(S[0][:, h, :], pm_[0:64, 0:TB], [Bp_, BS[0]], [BS[0]])
                op("dve", lambda e: e.tensor_tensor(S[1][:], yT[:], yT[:], ALU.mult), r=[ByT, BS[1]], w=[BS[1]])
                for h in range(8):
                    pm_, Bp_ = pmisc[h % 4], Bpm[h % 4]
                    op("pe", lambda e: e.matmul(pm_[0:64, 0:TB], lhsT=ones[:], rhs=yT[:, h, :], start=True, stop=True), r=[Bones, ByT], w=[Bp_], inc=False)
                    op("pe", lambda e: e.matmul(pm_[0:64, TB:2 * TB], lhsT=ones[:], rhs=S[1][:, h, :], start=True, stop=True), r=[Bones, BS[1]], w=[Bp_])
                    op("act", lambda e: e.activation(S[2][:, h, :], pm_[0:64, 0:TB], AF.Copy, scale=1.0 / 64), r=[Bp_, BS[2]], w=[BS[2]])
                    op("act", lambda e: e.activation(S[1][:, h, :], pm_[0:64, TB:2 * TB], AF.Copy, scale=1.0 / 64), r=[Bp_, BS[1]], w=[BS[1]])
                op("dve", lambda e: e.tensor_tensor(S[4][:], S[2][:], S[2][:], ALU.mult), r=[BS[2], BS[4]], w=[BS[4]])
                op("dve", lambda e: e.tensor_tensor(S[1][:], S[1][:], S[4][:], ALU.subtract), r=[BS[1], BS[4]], w=[BS[1]])
                op("dve", lambda e: e.tensor_scalar(S[1][:], S[1][:], 0.0, LNX_EPS, ALU.max, ALU.add), r=[BS[1]], w=[BS[1]])
                op("act", lambda e: e.activation(S[1][:], S[1][:], AF.Sqrt), r=[BS[1]], w=[BS[1]])
                op("dve", lambda e: e.reciprocal(S[1][:], S[1][:]), r=[BS[1]], w=[BS[1]])
                op("dve", lambda e: e.tensor_tensor(S[4][:], yT[:], S[2][:], ALU.subtract), r=[ByT, BS[2], BS[4]], w=[BS[4]])
                op("dve", lambda e: e.tensor_tensor(S[4][:], S[4][:], S[1][:], ALU.mult), r=[BS[4], BS[1]], w=[BS[4]])
                op("pool", lambda e: e.tensor_tensor(S[4][:], S[4][:], prm_b(6), ALU.mult), r=[BS[4], Bprm], w=[BS[4]])
                op("pool", lambda e: e.tensor_tensor(S[4][:], S[4][:], prm_b(7), ALU.add), r=[BS[4], Bprm], w=[BS[4]])
                op("dve", lambda e: e.tensor_tensor(S[4][:], S[4][:], S[3][:], ALU.add), r=[BS[4], BS[3]], w=[BS[4]])
                op("dve", lambda e: e.tensor_tensor(ysb[:], S[4][:], S[0][:], ALU.mult), r=[BS[4], BS[0], Bysb], w=[Bysb])
                fw.dma("sp", ysend.ap().rearrange("(h i) t -> i h t", i=64)[:, :, t0:t0 + TB], ysb[:], reads=[Bysb], writes=[BYS])
        fw.barrier()
        if dbg == 1:
            fw.dma("sp", dbg_y, ysend.ap(), reads=[BYS], writes=[Buf()])
            fw.barrier()
            return nc
        scope("B1")
        if not skip:
          op("pool", lambda e: e.collective_compute("AllGather", ALU.bypass, replica_groups=[list(range(8))],
                                                  ins=[ysend.ap().opt()], outs=[ygath.ap().opt()]), r=[BYS], w=[BYG])
        with ExitStack() as es:
            xnT = sbt(es, "b_xnT", [128, 32, NT + 2], BF16); BxnT = Buf()
            cT = sbt(es, "b_cT", [128, 16, NT], BF16); BcT = Buf()
            cw = sbt(es, "b_cw", [128, 48]); gbs = sbt(es, "b_gbs", [128, 64]); selt = sbt(es, "b_sel", [128, 8]); Bcst = Buf()
            fw.dma("sp", cw[:], convw, writes=[Bcst])
            fw.dma("sp", gbs[:], gbias, writes=[Bcst])
            fw.dma("sp", selt[:], sel, writes=[Bcst])
            with ExitStack() as es2:
                xt = sbt(es2, "b_xt", [128, D]); xs = sbt(es2, "b_xs", [128, D], BF16); st = sbt(es2, "b_st", [128, 8])
                pT = [pst(es2, f"b_pT{i}", [128, 512]) for i in range(2)]
                nbufs = (xt, Buf(), xs, Buf(), pT, [Buf(), Buf()], st, Buf())
                norm_T(nbufs, xprev, lambda k0, nk: (xnT[:, k0:k0 + nk, 0:2], BxnT), ncols_keep=2, col0=126)
                for tt in range(SK(NT // 128)):
                    norm_T(nbufs, xown[tt * 128:(tt + 1) * 128, :],
                           lambda k0, nk, tt=tt: (xnT[:, k0:k0 + nk, 2 + tt * 128:2 + (tt + 1) * 128], BxnT))
                fw.barrier()
            scope("B2")
            with ExitStack() as es2:
                wc = [sbt(es2, f"b_wc{i}", [128, 32, 3, 128], BF16) for i in range(2)]; Bwc = [Buf(), Buf()]
                pB = pst(es2, "b_pB", [128, 1024]); pC = pst(es2, "b_pC", [128, 1024]); pU = pst(es2, "b_pU", [128, 1024]); pH = pst(es2, "b_pH", [128, 512]); pH2 = pst(es2, "b_pH2", [128, 512])
                BpB, BpC, BpU, BpH = Buf(), Buf(), Buf(), Buf()
                uS = sbt(es2, "b_uS", [128, NT + 2]); zz = sbt(es2, "b_z", [128, NT + 2]); zc = sbt(es2, "b_zc", [128, NT]); BuS, Bz, Bzc = Buf(), Buf(), Buf()
                for ci in range(SK(16)):
                    w_, Bw_ = wc[ci % 2], Bwc[ci % 2]
                    for fam in range(3):
                        fw.dma("pool", w_[:, :, fam, :], w_conv[:, fam * 2048 + ci * 128:fam * 2048 + (ci + 1) * 128].rearrange("(k p) n -> p k n", p=128), writes=[Bw_])
                    for k in range(32):
                        st_, sp_ = (k == 0), (k == 31)
                        for hf in range(2):
                            op("pe", lambda e: e.matmul(pB[:, hf * 512:(hf + 1) * 512], lhsT=w_[:, k, 0, :], rhs=xnT[:, k, 2 + hf * 512:2 + (hf + 1) * 512], start=st_, stop=sp_),
                               r=[Bw_, BxnT], w=[BpB], inc=False)
                        for (fam, pp, Bpp, pHx) in ((1, pC, BpC, pH), (2, pU, BpU, pH2)):
                            for hf in range(2):
                                op("pe", lambda e: e.matmul(pp[:, hf * 512:(hf + 1) * 512], lhsT=w_[:, k, fam, :], rhs=xnT[:, k, 2 + hf * 512:2 + (hf + 1) * 512], start=st_, stop=sp_),
                                   r=[Bw_, BxnT], w=[Bpp], inc=False)
                            op("pe", lambda e: e.matmul(pHx[:, 0:2], lhsT=w_[:, k, fam, :], rhs=xnT[:, k, 0:2], start=st_, stop=sp_),
                               r=[Bw_, BxnT], w=[BpH, Bpp], inc=(fam == 2 and sp_))
                    op("act", lambda e: e.copy(uS[:, 2:NT + 2], pU[:]), r=[BpU, BuS], w=[BuS])
                    op("act", lambda e: e.copy(uS[:, 0:2], pH2[:, 0:2]), r=[BpH, BuS], w=[BuS])
                    op("dve", lambda e: e.tensor_tensor(zz[:, 2:NT + 2], pC[:], uS[:, 2:NT + 2], ALU.mult), r=[BpC, BuS, Bz], w=[Bz])
                    op("dve", lambda e: e.tensor_tensor(zz[:, 0:2], pH[:, 0:2], uS[:, 0:2], ALU.mult), r=[BpH, BuS, Bz], w=[Bz])
                    op("dve", lambda e: e.tensor_scalar(zc[:], zz[:, 2:NT + 2], cw[:, ci * 3 + 2:ci * 3 + 3], None, ALU.mult), r=[Bz, Bcst, Bzc], w=[Bzc])
                    op("dve", lambda e: e.scalar_tensor_tensor(zc[:], zz[:, 1:NT + 1], cw[:, ci * 3 + 1:ci * 3 + 2], zc[:], ALU.mult, ALU.add), r=[Bz, Bcst, Bzc], w=[Bzc])
                    op("dve", lambda e: e.scalar_tensor_tensor(zc[:], zz[:, 0:NT], cw[:, ci * 3:ci * 3 + 1], zc[:], ALU.mult, ALU.add), r=[Bz, Bcst, Bzc], w=[Bzc])
                    op("dve", lambda e: e.tensor_tensor(cT[:, ci, :], zc[:], pB[:], ALU.mult), r=[Bzc, BpB, BcT], w=[BcT])
                fw.barrier()
            scope("B3")
            ygT = sbt(es, "b_ygT", [128, 16, NT], BF16); BygT = Buf()
            with ExitStack() as es2:
                ytmp = sbt(es2, "b_ytmp", [128, 16, NT], BF16); Bytmp = Buf()
                for j in range(SK(8)):
                    fw.dma("sp", ytmp[:], ygath.ap()[(j // 4) * 2048:(j // 4 + 1) * 2048, (j % 4) * NT:(j % 4 + 1) * NT].rearrange("(c p) t -> p c t", p=128),
                           reads=[BYG], writes=[Bytmp])
                    for q in range(4):
                        o_ = ygT[:, q * 4:(q + 1) * 4, :]; i_ = ytmp[:, q * 4:(q + 1) * 4, :]
                        if j == 0:
                            op("dve", lambda e: e.tensor_scalar(o_, i_, selt[:, 0:1], None, ALU.mult), r=[Bytmp, Bcst, BygT], w=[BygT])
                        else:
                            op("dve", lambda e: e.scalar_tensor_tensor(o_, i_, selt[:, j:j + 1], o_, ALU.mult, ALU.add), r=[Bytmp, Bcst, BygT], w=[BygT])
                fw.barrier()
            scope("B4")
            with ExitStack() as es2:
                wg2 = [sbt(es2, f"b_wg{i}", [128, 32, 2, 128], BF16) for i in range(2)]; Bwg = [Buf(), Buf()]
                wab = [sbt(es2, f"b_wab{i}", [128, 16, 2, 128], BF16) for i in range(2)]; Bwab = [Buf(), Buf()]
                pYA = pst(es2, "b_pYA", [128, 1024]); pYB = pst(es2, "b_pYB", [128, 1024]); pGA = pst(es2, "b_pGA", [128, 1024]); pGB = pst(es2, "b_pGB", [128, 1024])
                BpYA, BpYB, BpGA, BpGB = Buf(), Buf(), Buf(), Buf()
                sa = sbt(es2, "b_sa", [128, NT]); sb_ = sbt(es2, "b_sb", [128, NT]); Bsa, Bsb = Buf(), Buf()
                mo = [sbt(es2, f"b_mo{i}", [128, NT], BF16) for i in range(2)]; Bmo = [Buf(), Buf()]
                for dc in range(SK(32)):
                    wg_, Bwg_ = wg2[dc % 2], Bwg[dc % 2]
                    wab_, Bwab_ = wab[dc % 2], Bwab[dc % 2]
                    for ab in range(2):
                        fw.dma("pool", wg_[:, :, ab, :], w_gate[:, ab * D + dc * 128:ab * D + (dc + 1) * 128].rearrange("(k p) n -> p k n", p=128), writes=[Bwg_])
                    fw.dma("pool", wab_[:, :, 0, :], w_oa[:, dc * 128:(dc + 1) * 128].rearrange("(k p) n -> p k n", p=128), writes=[Bwab_])
                    fw.dma("pool", wab_[:, :, 1, :], w_ob[:, dc * 128:(dc + 1) * 128].rearrange("(k p) n -> p k n", p=128), writes=[Bwab_])
                    for (pp, Bpp, src, Bsrc, ab) in ((pYA, BpYA, ygT, BygT, 0), (pYB, BpYB, cT, BcT, 1)):
                        for k in range(16):
                            for hf in range(2):
                                op("pe", lambda e: e.matmul(pp[:, hf * 512:(hf + 1) * 512], lhsT=wab_[:, k, ab, :], rhs=src[:, k, hf * 512:(hf + 1) * 512], start=(k == 0), stop=(k == 15)),
                                   r=[Bwab_, Bsrc], w=[Bpp], inc=(k == 15 and hf == 1))
                    for (pp, Bpp, ab) in ((pGA, BpGA, 0), (pGB, BpGB, 1)):
                        for k in range(32):
                            for hf in range(2):
                                op("pe", lambda e: e.matmul(pp[:, hf * 512:(hf + 1) * 512], lhsT=wg_[:, k, ab, :], rhs=xnT[:, k, 2 + hf * 512:2 + (hf + 1) * 512], start=(k == 0), stop=(k == 31)),
                                   r=[Bwg_, BxnT], w=[Bpp], inc=(k == 31 and hf == 1))
                    op("act", lambda e: e.activation(sa[:], pGA[:], AF.Sigmoid, bias=gbs[:, dc:dc + 1], scale=1.0), r=[BpGA, Bcst, Bsa], w=[Bsa])
                    op("act", lambda e: e.activation(sb_[:], pGB[:], AF.Sigmoid, bias=gbs[:, 32 + dc:33 + dc], scale=1.0), r=[BpGB, Bcst, Bsb], w=[Bsb])
                    op("dve", lambda e: e.tensor_tensor(sa[:], sa[:], pYA[:], ALU.mult), r=[Bsa, BpYA], w=[Bsa])
                    op("dve", lambda e: e.tensor_tensor(sb_[:], sb_[:], pYB[:], ALU.mult), r=[Bsb, BpYB], w=[Bsb])
                    mo_, Bmo_ = mo[dc % 2], Bmo[dc % 2]
                    op("dve", lambda e: e.tensor_tensor(mo_[:], sa[:], sb_[:], ALU.add), r=[Bsa, Bsb, Bmo_], w=[Bmo_])
                    fw.dma("sp", mscr.ap()[dc * 128:(dc + 1) * 128, :], mo_[:], reads=[Bmo_], writes=[BMS])
                fw.barrier()
        fw.barrier()
        if dbg == 2:
            fw.dma("sp", dbg_m, mscr.ap(), reads=[BMS], writes=[Buf()])
            fw.dma("sp", dbg_y, ysend.ap(), reads=[BYS], writes=[Buf()])
            fw.barrier()
            return nc
        scope("C")
        HT = 512
        for hf in range(NT // HT):
            with ExitStack() as es:
                h1 = sbt(es, f"c{hf}_h1", [128, 4, D]); Bh1 = [[Buf() for _ in range(8)] for _ in range(4)]
                for tl in range(4):
                    fw.dma("sp", h1[:, tl, :], xown[hf * HT + tl * 128:hf * HT + (tl + 1) * 128, :], writes=Bh1[tl])
                with ExitStack() as es2:
                    mTh = sbt(es2, f"c{hf}_mTh", [128, 32, HT], BF16); BmTh = Buf()
                    wo = [sbt(es2, f"c{hf}_wo{i}", [128, 32, 512], BF16) for i in range(2)]; Bwo = [Buf(), Buf()]
                    pc = [pst(es2, f"c{hf}_pc{i}", [128, 512]) for i in range(4)]; Bpc = [Buf() for _ in range(4)]
                    fw.dma("sp", mTh[:], mscr.ap()[:, hf * HT:(hf + 1) * HT].rearrange("(k p) t -> p k t", p=128), reads=[BMS], writes=[BmTh])
                    for cb in range(8):
                        wo_, Bwo_ = wo[cb % 2], Bwo[cb % 2]
                        if hf == 0:
                            for q in range(4):
                                fw.dma("pool", wo_[:, q * 8:(q + 1) * 8, :], w_o[q * 1024:(q + 1) * 1024, cb * 512:(cb + 1) * 512].rearrange("(k p) n -> p k n", p=128), writes=[Bwo_])
                            fw.dma("sp", wo_s.ap()[cb], wo_[:], reads=[Bwo_], writes=[Bwo_s[cb]])
                        else:
                            fw.dma("sp", wo_[:], wo_s.ap()[cb], reads=[Bwo_s[cb]], writes=[Bwo_])
                        for tl in range(4):
                            pp, Bpp = pc[tl], Bpc[tl]
                            for k in range(32):
                                op("pe", lambda e: e.matmul(pp[:], lhsT=mTh[:, k, tl * 128:(tl + 1) * 128], rhs=wo_[:, k, :], start=(k == 0), stop=(k == 31)),
                                   r=[BmTh, Bwo_], w=[Bpp], inc=(k == 31))
                            op("dve", lambda e: e.tensor_tensor(h1[:, tl, cb * 512:(cb + 1) * 512], h1[:, tl, cb * 512:(cb + 1) * 512], pp[:], ALU.add), r=[Bpp, Bh1[tl][cb]], w=[Bh1[tl][cb]])
                    fw.barrier()
                if dbg == 3:
                    for tl in range(4):
                        fw.dma("sp", out[tl * 128:(tl + 1) * 128, :], h1[:, tl, :], reads=Bh1[tl], writes=[Buf()])
                    fw.barrier()
                    return nc
                load_gb(nmlp)
                hnT = sbt(es, f"c{hf}_hnT", [128, 32, HT], BF16); BhnT = Buf()
                with ExitStack() as es2:
                    xs = sbt(es2, f"c{hf}_xs", [128, D], BF16); st = sbt(es2, f"c{hf}_st", [128, 8])
                    pT = [pst(es2, f"c{hf}_pT{i}", [128, 512]) for i in range(2)]
                    Bxs_c, BpT_c, Bst_c = Buf(), [Buf(), Buf()], Buf()
                    for tl in range(4):
                        nb = (h1[:, tl, :], Bh1[tl], xs, Bxs_c, pT, BpT_c, st, Bst_c)
                        norm_T_sb(nb, lambda k0, nk, tl=tl: (hnT[:, k0:k0 + nk, tl * 128:(tl + 1) * 128], BhnT))
                    fw.barrier()
                if dbg == 4:
                    fw.barrier()
                    return nc
                NFB = int(os.environ.get("KNFB", "64"))
                with ExitStack() as es2:
                    FB = 256
                    wup = [sbt(es2, f"c{hf}_wup{i}", [128, 32, FB], BF16) for i in range(2)]; Bwup = [Buf(), Buf()]
                    wdn = [sbt(es2, f"c{hf}_wdn{i}", [128, FB // 128, D], BF16) for i in range(2)]; Bwdn = [Buf(), Buf()]
                    uT = [sbt(es2, f"c{hf}_uT{i}", [128, FB // 128, HT], BF16) for i in range(2)]; BuT = [Buf(), Buf()]
                    rl = [sbt(es2, f"c{hf}_rl{i}", [128, HT]) for i in range(2)]; Brl = [Buf(), Buf()]
                    pu = [pst(es2, f"c{hf}_pu{i}", [128, 512]) for i in range(2)]; Bpu = [Buf(), Buf()]
                    pd = [pst(es2, f"c{hf}_pd{i}", [128, 512]) for i in range(6)]; Bpd = [Buf() for _ in range(6)]
                    ndn = [0]
                    NF = min(NFB, 4 * D // FB)

                    def load_up(fb):
                        wu_, Bwu_ = wup[fb % 2], Bwup[fb % 2]
                        if hf == 0:
                            for q in range(2):
                                fw.dma("pool", wu_[:, q * 16:(q + 1) * 16, :], w_up[q * 2048:(q + 1) * 2048, fb * FB:(fb + 1) * FB].rearrange("(k p) n -> p k n", p=128), writes=[Bwu_])
                            fw.dma("sp", wup_s.ap()[fb], wu_[:], reads=[Bwu_], writes=[Bwup_s[fb]])
                        else:
                            fw.dma("sp", wu_[:], wup_s.ap()[fb], reads=[Bwup_s[fb]], writes=[Bwu_])

                    def load_dn(fb):
                        wd_, Bwd_ = wdn[fb % 2], Bwdn[fb % 2]
                        if hf == 0:
                            fw.dma("pool", wd_[:], w_dn[fb * FB:(fb + 1) * FB, :].rearrange("(f p) n -> p f n", p=128), writes=[Bwd_])
                            fw.dma("sp", wdn_s.ap()[fb], wd_[:], reads=[Bwd_], writes=[Bwdn_s[fb]])
                        else:
                            fw.dma("sp", wd_[:], wdn_s.ap()[fb], reads=[Bwdn_s[fb]], writes=[Bwd_])

                    def up(fb):
                        wu_, Bwu_ = wup[fb % 2], Bwup[fb % 2]
                        u_, Bu_ = uT[fb % 2], BuT[fb % 2]
                        for fc in range(FB // 128):
                            pp, Bpp = pu[fc % 2], Bpu[fc % 2]
                            for k in range(32):
                                op("pe", lambda e: e.matmul(pp[:], lhsT=wu_[:, k, fc * 128:(fc + 1) * 128], rhs=hnT[:, k, :], start=(k == 0), stop=(k == 31)),
                                   r=[Bwu_, BhnT], w=[Bpp], inc=(k == 31))
                            r_, Br_ = rl[fc % 2], Brl[fc % 2]
                            op("act", lambda e: e.activation(r_[:], pp[:], AF.Relu), r=[Bpp, Br_], w=[Br_])
                            op("pool", lambda e: e.tensor_tensor(u_[:, fc, :], r_[:], r_[:], ALU.mult), r=[Br_, Bu_], w=[Bu_])

                    def down(fb):
                        wd_, Bwd_ = wdn[fb % 2], Bwdn[fb % 2]
                        u_, Bu_ = uT[fb % 2], BuT[fb % 2]
                        for tl in range(4):
                            for cb in range(8):
                                pp, Bpp = pd[ndn[0] % 6], Bpd[ndn[0] % 6]
                                ndn[0] += 1
                                for fc in range(FB // 128):
                                    op("pe", lambda e: e.matmul(pp[:], lhsT=u_[:, fc, tl * 128:(tl + 1) * 128], rhs=wd_[:, fc, cb * 512:(cb + 1) * 512],
                                                                start=(fc == 0), stop=(fc == FB // 128 - 1)), r=[Bu_, Bwd_], w=[Bpp], inc=(fc == FB // 128 - 1))
                                op("dve", lambda e: e.tensor_tensor(h1[:, tl, cb * 512:(cb + 1) * 512], h1[:, tl, cb * 512:(cb + 1) * 512], pp[:], ALU.add),
                                   r=[Bpp, Bh1[tl][cb]], w=[Bh1[tl][cb]])

                    for fb in range(NF):
                        load_up(fb)
                        load_dn(fb)
                        up(fb)
                        down(fb)
                    fw.barrier()
                if dbg == 5:
                    for tl in range(4):
                        fw.dma("sp", out[tl * 128:(tl + 1) * 128, :], h1[:, tl, :], reads=Bh1[tl], writes=[Buf()])
                    fw.barrier()
                    return nc
                load_gb(nfin)
                with ExitStack() as es2:
                    junk = sbt(es2, f"c{hf}_junk", [128, D], BF16); st = sbt(es2, f"c{hf}_st2", [128, 8]); Bj, Bst = Buf(), Buf()
                    for tl in range(4):
                        op("act", lambda e: e.activation(junk[:], h1[:, tl, :], AF.Square, accum_out=st[:, 0:1]), r=Bh1[tl] + [Bj, Bst], w=[Bj, Bst])
                        op("dve", lambda e: e.tensor_scalar(st[:, 1:2], st[:, 0:1], 1.0 / D, RMS_EPS, ALU.mult, ALU.add), r=[Bst], w=[Bst])
                        op("act", lambda e: e.activation(st[:, 2:3], st[:, 1:2], AF.Sqrt), r=[Bst], w=[Bst])
                        op("dve", lambda e: e.reciprocal(st[:, 3:4], st[:, 2:3]), r=[Bst], w=[Bst])
                        op("dve", lambda e: e.scalar_tensor_tensor(h1[:, tl, :], h1[:, tl, :], st[:, 3:4], gb[:], ALU.mult, ALU.mult), r=Bh1[tl] + [Bst, Bgb], w=Bh1[tl])
                        fw.dma("sp", out[hf * HT + tl * 128:hf * HT + (tl + 1) * 128, :], h1[:, tl, :], reads=Bh1[tl], writes=[Buf()])
                    fw.barrier()
        fw.barrier()
        scope(None)
    return nc


def make_in_maps(inp):
    f = lambda a: np.ascontiguousarray(a, dtype=np.float32)
    x = inp["x"]
    w_in = inp["w_in"][0]
    mu = inp["shift_mu"][0]
    NR = 6592
    w_conv = f(w_in[:, NR:NR + 6144])
    w_gate = f(w_in[:, NR + 6144:])
    gbias = f(inp["gate_bias"][0].reshape(64, 128).T)
    convw = f(inp["conv_w"][0].reshape(3, 16, 128).transpose(2, 1, 0).reshape(128, 48))
    common = {
        "nmix": f(inp["norm_mix_w"]), "nmlp": f(inp["norm_mlp_w"]), "nfin": f(inp["norm_final_w"].reshape(1, D)),
        "w_conv": w_conv, "w_gate": w_gate, "gbias": gbias, "convw": convw,
        "w_oa": f(inp["w_out_a"][0]), "w_ob": f(inp["w_out_b"][0]), "w_o": f(inp["w_out"][0]),
        "w_up": f(inp["w_mlp_up"][0]), "w_dn": f(inp["w_mlp_down"][0]),
        "mu_wd": f(mu[6144:6240].reshape(96, 1)), "mu_ad": f(mu[6240:6336].reshape(96, 1)),
        "mu_gd": f(mu[6336:6592].reshape(2, 128).T),
    }
    maps = []
    for c in range(8):
        b, g = c // 4, c % 4
        hs = slice(g * 512, (g + 1) * 512)
        cols = np.concatenate([np.arange(fam * 2048 + g * 512, fam * 2048 + (g + 1) * 512) for fam in range(3)] + [np.arange(6144, 6592)])
        hj = lambda v: v[hs].reshape(8, 64).T
        ka = inp["k_a"][0]
        p_hj = np.concatenate([hj(inp["w0"][0]), hj(inp["a0"][0]), hj(inp["k_k"][0]), hj(ka), hj(ka),
                               hj(inp["r_k"][0].reshape(-1)), hj(inp["lnx_w"][0]), hj(inp["lnx_b"][0])], axis=1)
        mu_hj = np.concatenate([mu[fam * 2048:(fam + 1) * 2048][hs].reshape(8, 64).T for fam in range(3)], axis=1)
        xprev = np.zeros((128, D), np.float32)
        if g > 0:
            xprev[:] = x[b, g * NT - 128:g * NT]
        selv = np.zeros((128, 8), np.float32)
        selv[:, c] = 1.0
        m = dict(common)
        m.update({
            "xseq": f(x[b]), "xown": f(x[b, g * NT:(g + 1) * NT]), "xprev": xprev,
            "w_rwkv": f(w_in[:, cols]), "mu_hj": f(mu_hj), "p_hj": f(p_hj),
            "wdu": f(inp["w_decay_up"][0][:, hs]), "wau": f(inp["w_iclr_up"][0][:, hs]), "wgu": f(inp["w_gate_up"][0][:, hs]),
            "sel": selv,
        })
        maps.append(m)
    return maps


_NC_CACHE = {}


def kernel(**inputs):
    dbg = int(os.environ.get("KDBG", "0"))
    inp = {k: np.asarray(v) for k, v in inputs.items()}
    maps = make_in_maps(inp)
    if dbg not in _NC_CACHE:
        _NC_CACHE[dbg] = build_program(dbg)
    nc = _NC_CACHE[dbg]
    res = run_bass_kernel_spmd(nc, maps, core_ids=list(range(8)))
    outp = np.empty((2, T, D), np.float32)
    for c in range(8):
        b, g = c // 4, c % 4
        outp[b, g * NT:(g + 1) * NT] = res.results[c]["out"]
    if dbg:
        kernel.dbg = res.results
    return outp
```
